# Optimizing a Trainium2 kernel written in Bass

```python
import math
import jax, jax.numpy as jnp
from jax import lax
import numpy as np

D_MODEL = 2048
BATCH = 1
SEQ = 8192
DEPTH = 4

CHUNK = 64
Q_BLOCK = 128
ROPE_THETA = 500000.0
NORM_EPS = 1e-6

LRU_WIDTH = 768
LRU_BLOCKS = 6
LRU_BLOCK_W = LRU_WIDTH // LRU_BLOCKS
CONV_W = 4
LRU_C = 8.0

DIFF_HEADS = 4
DIFF_HEAD_DIM = 64
DIFF_V_DIM = 2 * DIFF_HEAD_DIM
DIFF_QK = DIFF_HEADS * 2 * DIFF_HEAD_DIM
DIFF_OUT = DIFF_HEADS * DIFF_V_DIM
DIFF_ROT = DIFF_HEAD_DIM // 4
SUBLN_EPS = 1e-5

MLA_HEADS = 6
MLA_NOPE = 128
MLA_ROPE = 64
MLA_V = 128
MLA_Q_RANK = 512
MLA_KV_RANK = 256
MLA_OUT = MLA_HEADS * MLA_V

MIX_WIDTH = LRU_WIDTH + DIFF_OUT + MLA_OUT
IN_SIZES = (LRU_WIDTH, LRU_WIDTH, DIFF_QK, DIFF_QK, DIFF_OUT, MLA_Q_RANK, MLA_KV_RANK + MLA_ROPE)
IN_WIDTH = 768 + 768 + 512 + 512 + 512 + 512 + 320
IN_SPLITS = (768, 1536, 2048, 2560, 3072, 3584)

D_FF = -(-8 * D_MODEL // (3 * 256)) * 256

kernel_name = "hybrid_rglru_diffattn_mla_block"


def rmsnorm(x, g, eps=NORM_EPS):
    xf = x.astype(jnp.float32)
    y = xf * lax.rsqrt(jnp.mean(xf * xf, axis=-1, keepdims=True) + eps)
    return (y * g.astype(jnp.float32)).astype(x.dtype)


def rope(x, positions, rot_dim):
    half = rot_dim // 2
    inv_freq = ROPE_THETA ** (-jnp.arange(half, dtype=jnp.float32) / half)
    ang = positions.astype(jnp.float32)[..., None] * inv_freq
    ang = ang.reshape(ang.shape[:2] + (1,) * (x.ndim - 3) + (half,))
    cos = jnp.cos(ang).astype(x.dtype)
    sin = jnp.sin(ang).astype(x.dtype)
    x1 = x[..., :half]
    x2 = x[..., half:rot_dim]
    return jnp.concatenate([x1 * cos - x2 * sin, x2 * cos + x1 * sin, x[..., rot_dim:]], axis=-1)


def chunk_causal_attention(q, k, v, coeff, scale):
    B, S, H, M, d = q.shape
    dv = v.shape[-1]
    n_blk = S // Q_BLOCK
    key_chunk = jnp.arange(S) // CHUNK
    kf = k.astype(jnp.float32)
    vf = v.astype(jnp.float32)
    cf = coeff.astype(jnp.float32)
    q_blocks = jnp.moveaxis(q.reshape(B, n_blk, Q_BLOCK, H, M, d), 1, 0)

    def one_block(args):
        q_blk, blk = args
        s = jnp.einsum('bqhmd,bkhmd->bhmqk', q_blk.astype(jnp.float32), kf) * scale
        q_chunk = (blk * Q_BLOCK + jnp.arange(Q_BLOCK)) // CHUNK
        mask = key_chunk[None, :] <= q_chunk[:, None]
        s = jnp.where(mask, s, -1e30)
        p = jax.nn.softmax(s, axis=-1)
        w = jnp.einsum('bhmqk,m->bhqk', p, cf)
        return jnp.einsum('bhqk,bkhd->bqhd', w, vf)

    out = lax.map(one_block, (q_blocks, jnp.arange(n_blk)))
    return jnp.moveaxis(out, 0, 1).reshape(B, S, H, dv).astype(v.dtype)


def _lru_combine(c1, c2):
    a1, b1 = c1
    a2, b2 = c2
    return a1 * a2, a2 * b1 + b2


def rglru_group(xb, yb, conv_w, conv_b, w_r, b_r, w_i, b_i, lru_lambda):
    B, S, C = xb.shape
    xc = lax.conv_general_dilated(
        xb, conv_w[:, None, :].astype(xb.dtype), window_strides=(1,),
        padding=[(CONV_W - 1, 0)], dimension_numbers=('NWC', 'WIO', 'NWC'),
        feature_group_count=C) + conv_b
    xh = xc.reshape(B, S, LRU_BLOCKS, LRU_BLOCK_W)
    r = jax.nn.sigmoid(jnp.einsum('bshc,hcd->bshd', xh, w_r).reshape(B, S, C) + b_r)
    i = jax.nn.sigmoid(jnp.einsum('bshc,hcd->bshd', xh, w_i).reshape(B, S, C) + b_i)
    log_a = -LRU_C * r.astype(jnp.float32) * jax.nn.softplus(-lru_lambda.astype(jnp.float32))
    a = jnp.exp(log_a)
    b = jnp.sqrt(-jnp.expm1(2.0 * log_a)) * (i * xc).astype(jnp.float32)
    _, h = lax.associative_scan(_lru_combine, (a, b), axis=1)
    return h.astype(xb.dtype) * jax.nn.gelu(yb)


def diff_attention_group(dq, dk, dv, positions, lam_q1, lam_k1, lam_q2, lam_k2, g_sub, lambda_init):
    B, S, _ = dq.shape
    q = rope(dq.reshape(B, S, DIFF_HEADS, 2, DIFF_HEAD_DIM), positions, DIFF_ROT)
    k = rope(dk.reshape(B, S, DIFF_HEADS, 2, DIFF_HEAD_DIM), positions, DIFF_ROT)
    v = dv.reshape(B, S, DIFF_HEADS, DIFF_V_DIM)
    lam = (jnp.exp(jnp.sum(lam_q1.astype(jnp.float32) * lam_k1.astype(jnp.float32)))
           - jnp.exp(jnp.sum(lam_q2.astype(jnp.float32) * lam_k2.astype(jnp.float32)))
           + lambda_init)
    coeff = jnp.stack([jnp.ones((), jnp.float32), -lam])
    o = chunk_causal_attention(q, k, v, coeff, DIFF_HEAD_DIM ** -0.5)
    o = rmsnorm(o, g_sub, eps=SUBLN_EPS) * (1.0 - lambda_init)
    return o.reshape(B, S, DIFF_OUT)


def mla_group(q_a, kv_a, positions, g_q_a, w_q_b, g_kv_a, w_kv_b):
    B, S, _ = q_a.shape
    q = (rmsnorm(q_a, g_q_a) @ w_q_b).reshape(B, S, MLA_HEADS, MLA_NOPE + MLA_ROPE)
    q = jnp.concatenate([q[..., :MLA_NOPE], rope(q[..., MLA_NOPE:], positions, MLA_ROPE)], axis=-1)
    kv_c = kv_a[..., :MLA_KV_RANK]
    k_rope = rope(kv_a[..., MLA_KV_RANK:], positions, MLA_ROPE)
    kv = (rmsnorm(kv_c, g_kv_a) @ w_kv_b).reshape(B, S, MLA_HEADS, MLA_NOPE + MLA_V)
    k = jnp.concatenate(
        [kv[..., :MLA_NOPE], jnp.broadcast_to(k_rope[:, :, None, :], (B, S, MLA_HEADS, MLA_ROPE))], axis=-1)
    v = kv[..., MLA_NOPE:]
    o = chunk_causal_attention(q[:, :, :, None, :], k[:, :, :, None, :], v,
                               jnp.ones((1,), jnp.float32), (MLA_NOPE + MLA_ROPE) ** -0.5)
    return o.reshape(B, S, MLA_OUT)


def setup_inputs(seed: int = 0) -> dict:
    key = jax.random.key(seed)
    ks = jax.random.split(key, 32)

    def nrm(k, shape, scale):
        return jax.random.normal(k, shape, jnp.float32) * scale

    def gain(k, shape):
        return 1.0 + 0.01 * jax.random.normal(k, shape, jnp.float32)

    x = jax.random.normal(ks[0], (BATCH, SEQ, D_MODEL), jnp.float32)
    offset = jax.random.randint(ks[1], (BATCH, 1), 0, 4096, dtype=jnp.int32)
    positions = offset + jnp.arange(SEQ, dtype=jnp.int32)[None, :]

    a0 = jax.random.uniform(ks[11], (DEPTH, LRU_WIDTH), jnp.float32, 0.9, 0.999)
    s0 = a0 ** (1.0 / LRU_C)
    lru_lambda = jnp.log(s0) - jnp.log1p(-s0)

    return {
        "x": x,
        "positions": positions,
        "g_mix": gain(ks[2], (DEPTH, D_MODEL)),
        "w_in": nrm(ks[3], (DEPTH, D_MODEL, IN_WIDTH), D_MODEL ** -0.5),
        "conv_w": nrm(ks[4], (DEPTH, CONV_W, LRU_WIDTH), CONV_W ** -0.5),
        "conv_b": nrm(ks[5], (DEPTH, LRU_WIDTH), 0.01),
        "w_r": nrm(ks[6], (DEPTH, LRU_BLOCKS, LRU_BLOCK_W, LRU_BLOCK_W), LRU_BLOCK_W ** -0.5),
        "b_r": nrm(ks[7], (DEPTH, LRU_WIDTH), 0.01),
        "w_i": nrm(ks[8], (DEPTH, LRU_BLOCKS, LRU_BLOCK_W, LRU_BLOCK_W), LRU_BLOCK_W ** -0.5),
        "b_i": nrm(ks[9], (DEPTH, LRU_WIDTH), 0.01),
        "lru_lambda": lru_lambda,
        "lam_q1": nrm(ks[12], (DEPTH, DIFF_HEAD_DIM), 0.1),
        "lam_k1": nrm(ks[13], (DEPTH, DIFF_HEAD_DIM), 0.1),
        "lam_q2": nrm(ks[14], (DEPTH, DIFF_HEAD_DIM), 0.1),
        "lam_k2": nrm(ks[15], (DEPTH, DIFF_HEAD_DIM), 0.1),
        "g_sub": gain(ks[16], (DEPTH, DIFF_V_DIM)),
        "g_q_a": gain(ks[17], (DEPTH, MLA_Q_RANK)),
        "w_q_b": nrm(ks[18], (DEPTH, MLA_Q_RANK, MLA_HEADS * (MLA_NOPE + MLA_ROPE)), MLA_Q_RANK ** -0.5),
        "g_kv_a": gain(ks[19], (DEPTH, MLA_KV_RANK)),
        "w_kv_b": nrm(ks[20], (DEPTH, MLA_KV_RANK, MLA_HEADS * (MLA_NOPE + MLA_V)), MLA_KV_RANK ** -0.5),
        "w_out": nrm(ks[21], (DEPTH, MIX_WIDTH, D_MODEL), MIX_WIDTH ** -0.5),
        "g_ffn": gain(ks[22], (DEPTH, D_MODEL)),
        "w_gate": nrm(ks[23], (DEPTH, D_MODEL, D_FF), D_MODEL ** -0.5),
        "w_up": nrm(ks[24], (DEPTH, D_MODEL, D_FF), D_MODEL ** -0.5),
        "w_down": nrm(ks[25], (DEPTH, D_FF, D_MODEL), D_FF ** -0.5),
        "g_final": gain(ks[26], (D_MODEL,)),
    }


def reference(x, positions, g_mix, w_in, conv_w, conv_b, w_r, b_r, w_i, b_i, lru_lambda,
              lam_q1, lam_k1, lam_q2, lam_k2, g_sub, g_q_a, w_q_b, g_kv_a, w_kv_b,
              w_out, g_ffn, w_gate, w_up, w_down, g_final):
    for l in range(DEPTH):
        lambda_init = 0.8 - 0.6 * math.exp(-0.3 * l)
        h = rmsnorm(x, g_mix[l])
        proj = h @ w_in[l]
        lru_x, lru_y, dq, dk, dv, q_a, kv_a = jnp.split(proj, IN_SPLITS, axis=-1)
        out_a = rglru_group(lru_x, lru_y, conv_w[l], conv_b[l], w_r[l], b_r[l], w_i[l], b_i[l], lru_lambda[l])
        out_b = diff_attention_group(dq, dk, dv, positions, lam_q1[l], lam_k1[l], lam_q2[l], lam_k2[l],
                                     g_sub[l], lambda_init)
        out_c = mla_group(q_a, kv_a, positions, g_q_a[l], w_q_b[l], g_kv_a[l], w_kv_b[l])
        mix = jnp.concatenate([out_a, out_b, out_c], axis=-1)
        x = x + mix @ w_out[l]
        h = rmsnorm(x, g_ffn[l])
        x = x + (jax.nn.silu(h @ w_gate[l]) * (h @ w_up[l])) @ w_down[l]
    return rmsnorm(x, g_final)
```

```python
import math
from contextlib import ExitStack

import numpy as np
import ml_dtypes
import concourse.bass as bass
import concourse.mybir as mybir
from concourse.bass_utils import run_bass_kernel_spmd

F32 = mybir.dt.float32
BF16 = mybir.dt.bfloat16
I32 = mybir.dt.int32
AF = mybir.ActivationFunctionType
ALU = mybir.AluOpType

NCORES = 8
D = 2048
S = 8192
T = 1024
NTT = 2
TT = 512
DEPTH = 4
DFF = 5632
INW = 3904
KVROWS = 2624
QROWS = 1664
BIG = 30000.0
NV = 88
ENGS = ("pe", "act", "dve", "pool", "sp")
EPOCH = 20000
DMA_K = 8


class Op:
    __slots__ = ("uid", "eng", "emit", "deps", "signals", "is_dma", "sem", "val", "pre", "_inc")

    def __init__(self, uid, eng, emit, is_dma):
        self.uid = uid
        self.eng = eng
        self.emit = emit
        self.deps = []
        self.signals = False
        self.is_dma = is_dma
        self.sem = None
        self.val = None
        self.pre = None
        self._inc = 16


class Prog:
    def __init__(self, nc):
        self.nc = nc
        self.ops = []
        self.streams = {e: [] for e in ENGS}
        self.last_w = {}
        self.readers = {}
        self.es = ExitStack()
        self.cc_ops = set()

    def sbuf(self, name, shape, dtype):
        return self.es.enter_context(self.nc.sbuf_tensor(name, list(shape), dtype))

    def psum(self, name, shape, dtype):
        return self.es.enter_context(self.nc.psum_tensor(name, list(shape), dtype))

    def _add(self, eng, emit, reads, writes, is_dma):
        op = Op(len(self.ops), eng, emit, is_dma)
        deps = set()
        for k in reads:
            w = self.last_w.get(k)
            if w is not None:
                deps.add(w)
        for k in writes:
            w = self.last_w.get(k)
            if w is not None:
                deps.add(w)
            for r in self.readers.get(k, ()):
                deps.add(r)
        best = {}
        for d in deps:
            dop = self.ops[d]
            if dop.is_dma:
                op.deps.append(d)
                dop.signals = True
                continue
            if dop.eng == eng and eng == "pe" and not is_dma:
                continue
            if dop.eng not in best or best[dop.eng] < d:
                best[dop.eng] = d
        for d in best.values():
            op.deps.append(d)
            self.ops[d].signals = True
        for k in writes:
            self.last_w[k] = op.uid
            self.readers[k] = []
        for k in reads:
            if k not in writes:
                self.readers.setdefault(k, []).append(op.uid)
        self.ops.append(op)
        self.streams[eng].append(op)
        return op

    def op(self, eng, emit, reads=(), writes=()):
        return self._add(eng, emit, tuple(reads), tuple(writes), False)

    def dma(self, queue, out, in_, reads=(), writes=()):
        def emit(e):
            return e.dma_start(out=out, in_=in_)
        o = self._add(queue, emit, tuple(reads), tuple(writes), True)
        o.signals = True
        return o

    def custom_dma(self, queue, emit, reads=(), writes=(), inc=16):
        o = self._add(queue, emit, tuple(reads), tuple(writes), True)
        o.signals = True
        o._inc = inc
        self.cc_ops.add(o.uid)
        return o

    def build(self):
        nc = self.nc
        es = self.es
        esems = {}
        for e in ENGS:
            n = sum(1 for o in self.streams[e] if o.signals and not o.is_dma)
            esems[e] = [es.enter_context(nc.semaphore(f"s_{e}_{i}")) for i in range(n // EPOCH + 1)]
        dsems = {}
        for e in ENGS:
            if any(o.is_dma for o in self.streams[e]):
                dsems[e] = [es.enter_context(nc.semaphore(f"d_{e}_{i}")) for i in range(DMA_K)]
        NCC = max(1, len(self.cc_ops))
        csems = [es.enter_context(nc.semaphore(f"cc_{i}")) for i in range(NCC)]
        ccount = [0] * NCC
        ncc = 0
        for e in ENGS:
            cnt = 0
            nd = 0
            dcount = [0] * DMA_K
            for o in self.streams[e]:
                if o.uid in self.cc_ops:
                    slot = ncc % NCC
                    if ccount[slot] > 0:
                        o.pre = (csems[slot], ccount[slot])
                    ccount[slot] += o._inc
                    o.sem = csems[slot]
                    o.val = ccount[slot]
                    ncc += 1
                elif o.is_dma:
                    slot = nd % DMA_K
                    if dcount[slot] > 0:
                        o.pre = (dsems[e][slot], dcount[slot])
                    dcount[slot] += o._inc
                    o.sem = dsems[e][slot]
                    o.val = dcount[slot]
                    nd += 1
                elif o.signals:
                    o.sem = esems[e][cnt // EPOCH]
                    o.val = cnt % EPOCH + 1
                    cnt += 1
        block = es.enter_context(nc.Block())
        ops = self.ops

        def run_stream(ename, eng):
            waited = {}
            for o in self.streams[ename]:
                need = {}
                for d in o.deps:
                    p = ops[d]
                    key = id(p.sem)
                    if waited.get(key, 0) >= p.val:
                        continue
                    if key not in need or need[key][1] < p.val:
                        need[key] = (p.sem, p.val)
                if o.pre is not None:
                    key = id(o.pre[0])
                    if waited.get(key, 0) < o.pre[1]:
                        if key not in need or need[key][1] < o.pre[1]:
                            need[key] = o.pre
                for key, (s, v) in need.items():
                    eng.wait_ge(s, v)
                    waited[key] = v
                ins = o.emit(eng)
                if o.is_dma:
                    ins.then_inc(o.sem, o._inc)
                elif o.signals:
                    ins.then_inc(o.sem, 1)
            if ename in dsems:
                tot = {}
                for o in self.streams[ename]:
                    if o.is_dma:
                        tot[id(o.sem)] = (o.sem, o.val)
                for key, (s, v) in tot.items():
                    if waited.get(key, 0) < v:
                        eng.wait_ge(s, v)

        if self.streams["sp"]:
            @block.sync
            def _(e):
                run_stream("sp", e)
        if self.streams["act"]:
            @block.scalar
            def _(e):
                run_stream("act", e)
        if self.streams["pe"]:
            @block.tensor
            def _(e):
                run_stream("pe", e)
        if self.streams["dve"]:
            @block.vector
            def _(e):
                run_stream("dve", e)
        if self.streams["pool"]:
            @block.gpsimd
            def _(e):
                run_stream("pool", e)
        es.close()
        return nc


class Ring:
    def __init__(self, items):
        self.items = items
        self.i = 0

    def next(self):
        it = self.items[self.i % len(self.items)]
        self.i += 1
        return it


def build_program(depth=DEPTH, debug=False):
    nc = bass.Bass("TRN2", target_bir_lowering=False)
    P = Prog(nc)

    def din(name, shape, dt):
        return nc.dram_tensor(name, list(shape), dt, kind="ExternalInput")

    xT_in = din("xT_in", [D, T], F32)
    pos_in = din("pos_in", [128, T], I32)
    consts_in = din("consts_in", [128, 8], F32)
    U_in = din("U_in", [16, S], BF16)
    Vm_in = din("Vm_in", [16, T], BF16)
    sel_in = din("sel_in", [128, 16], F32)
    vecs_in = din("vecs_in", [depth, 128, NV], F32)
    lams_in = din("lams_in", [depth, 128, 256], F32)
    gfin_in = din("gfin_in", [128, 16], F32)
    w_in = din("w_in", [depth, D, INW], F32)
    w_r = din("w_r", [depth, 6, 128, 128], F32)
    w_i = din("w_i", [depth, 6, 128, 128], F32)
    w_q_b = din("w_q_b", [depth, 512, 1152], F32)
    w_kv_b = din("w_kv_b", [depth, 256, 1536], F32)
    w_out = din("w_out", [depth, D, D], F32)
    w_gate = din("w_gate", [depth, D, DFF], F32)
    w_up = din("w_up", [depth, D, DFF], F32)
    w_down = din("w_down", [depth, DFF, D], F32)
    outT = nc.dram_tensor("outT", [D, T], F32, kind="ExternalOutput")

    KDR, KMR = 1024, 1600
    kvd_loc = nc.dram_tensor("kvd_loc", [KDR, T], BF16)
    kvd_all = [nc.dram_tensor(f"kvd_all{i}", [NCORES * KDR, T], BF16) for i in range(2)]
    kvm_loc = nc.dram_tensor("kvm_loc", [KMR, T], BF16)
    kvm_all = [nc.dram_tensor(f"kvm_all{i}", [NCORES * KMR, T], BF16) for i in range(2)]
    q_scr = nc.dram_tensor("q_scr", [QROWS, T], BF16)
    halo_loc = nc.dram_tensor("halo_loc", [128, 32], BF16)
    halo_all = [nc.dram_tensor(f"halo_all{i}", [NCORES * 128, 32], BF16) for i in range(2)]
    carry_loc = nc.dram_tensor("carry_loc", [128, 16], F32)
    carry_all = [nc.dram_tensor(f"carry_all{i}", [NCORES * 128, 16], F32) for i in range(2)]

    xT = P.sbuf("xT", [128, 16, T], F32)
    r1 = P.sbuf("r1", [128, 16, T], BF16)
    ropeT = P.sbuf("ropeT", [128, 4, T], BF16)
    lru = P.sbuf("lru", [128, 6 * 1028 + 6144], BF16)
    LXW = 1028
    def lx_ap(ch, a, b):
        return lru[:, ch * LXW + a: ch * LXW + b]
    def gy_ap(ch, a, b):
        return lru[:, 6 * LXW + ch * 1024 + a: 6 * LXW + ch * 1024 + b]
    def p1_ap(ch, a, b):
        return lru[:, ch * LXW + a: ch * LXW + b]
    def g_ap(j, a, b):
        return lru[:, j * 1024 + a: j * 1024 + b]
    NRING = 3
    ring_t = [P.sbuf(f"wring{i}", [128, 2048], BF16) for i in range(NRING)]
    wring = Ring([(ring_t[i], f"wring{i}") for i in range(NRING)])
    swt = [P.sbuf(f"swt{i}", [128, 2048], BF16) for i in range(1)]
    swring = Ring([(swt[i], f"swt{i}") for i in range(1)])
    NSC = 9
    sc_t = [P.sbuf(f"sc{i}", [128, TT], F32) for i in range(NSC)]
    scr = Ring([(sc_t[i], f"sc{i}") for i in range(NSC)])
    scr_att = Ring([(sc_t[i], f"sc{i}") for i in range(6, NSC)])
    NLT = 6
    lt_t = sc_t[:NLT]
    NBS = 3
    bs_t = [P.sbuf(f"bs{i}", [128, TT], BF16) for i in range(NBS)]
    bsr = Ring([(bs_t[i], f"bs{i}") for i in range(NBS)])
    NPT = 4
    pt_t = [P.sbuf(f"pt{i}", [128, TT], BF16) for i in range(NPT)]
    ptr = Ring([(pt_t[i], f"pt{i}") for i in range(NPT)])
    NST = 2
    kst = [P.sbuf(f"kst{i}", [128, 2, T], BF16) for i in range(NST)]
    vst = [P.sbuf(f"vst{i}", [128, 8, 128], BF16) for i in range(NST)]
    strr = Ring(list(range(NST)))
    qst = [P.sbuf(f"qst{i}", [128, 2, TT], BF16) for i in range(2)]
    qstr = Ring([0, 1])
    qn = P.sbuf("qn", [128, 4, T], BF16)
    kvn = qn
    htail = P.sbuf("htail", [128, 32], F32)
    ones_t = P.sbuf("ones", [128, TT], BF16)
    ones = ones_t[:, 0:128]
    vecs = P.sbuf("vecs", [128, depth * NV], F32)
    lamt = P.sbuf("lamt", [128, 256], F32)
    gfin = P.sbuf("gfin", [128, 16], F32)
    consts = P.sbuf("consts", [128, 8], F32)
    sel = P.sbuf("sel", [128, 16], F32)
    small = P.sbuf("small", [128, 64], F32)
    wg = P.sbuf("wg", [128, 12, 128], BF16)
    halt = P.sbuf("halt", [128, 8, 32], BF16)
    hsel = P.sbuf("hsel", [128, 32], F32)
    cart = P.sbuf("cart", [128, 8, 16], F32)
    carl = P.sbuf("carl", [128, 16], F32)

    ps = [P.psum(f"ps{i}", [128, TT], F32) for i in range(8)]
    psr = Ring([4, 5, 6, 7])
    psr2 = Ring([2, 3])

    def pk(i):
        return f"ps{i}"

    def act(out, in_, func, reads, writes, bias=0.0, scale=1.0):
        P.op("act", lambda e: e.activation(out=out, in_=in_, func=func, bias=bias, scale=scale), reads, writes)

    def tt_op(eng, out, a, b, op, reads, writes):
        P.op(eng, lambda e: e.tensor_tensor(out=out, in0=a, in1=b, op=op), reads, writes)

    def ts_op(eng, out, a, s1, s2, op0, op1, reads, writes):
        P.op(eng, lambda e: e.tensor_scalar(out=out, in0=a, scalar1=s1, scalar2=s2, op0=op0, op1=op1), reads, writes)

    def stt_op(out, a, s, b, op0, op1, reads, writes):
        P.op("dve", lambda e: e.scalar_tensor_tensor(out=out, in0=a, scalar=s, in1=b, op0=op0, op1=op1), reads, writes)

    def mm(out, lhsT, rhs, start, stop, reads, writes):
        P.op("pe", lambda e: e.matmul(out, lhsT, rhs, start=start, stop=stop), reads, writes)

    def tsl(tt):
        return slice(tt * TT, (tt + 1) * TT)

    for c in range(16):
        P.dma("sp", xT[:, c, :], xT_in[c * 128:(c + 1) * 128, :], writes=[("xT", c, 0), ("xT", c, 1)])
    P.dma("sp", vecs[:].rearrange("p (l v) -> p l v", l=depth), vecs_in.ap().rearrange("l p v -> p l v"), writes=["vecs"])
    P.dma("sp", gfin[:], gfin_in[:, :], writes=["gfin"])
    P.dma("sp", consts[:], consts_in[:, :], writes=["consts"])
    P.dma("sp", sel[:], sel_in[:, :], writes=["sel"])
    P.op("pool", lambda e: e.memset(ones_t[:], 1.0), writes=["ones"])
    posi = lru[:, 0:2048].bitcast(I32)
    ki_all = lru[:, 2048:3072].bitcast(I32)
    P.op("pool", lambda e: e.memset(htail[:], 0.0), writes=["htail"])
    P.dma("sp", posi[:], pos_in[:, :], writes=["posi"])
    posf = sc_t
    TWO_PI = 2.0 * math.pi
    for typ in range(2):
        invf = consts[:, typ:typ + 1]
        sgn = consts[:, 2 + 2 * typ: 3 + 2 * typ]
        msk = consts[:, 3 + 2 * typ: 4 + 2 * typ]
        for tt in range(NTT):
            for which in range(2):
                a_t, a_k = scr.next()
                k_t, k_k = scr.next()
                ki_t = ki_all
                P.op("dve", lambda e, a_t=a_t, tt=tt: e.tensor_copy(out=a_t[:], in_=posi[:, tsl(tt)]), ["posi"], [a_k])
                ts_op("dve", a_t[:], a_t[:], invf, (math.pi / 2 if which == 0 else 0.0), ALU.mult, ALU.add, [a_k, "consts"], [a_k])
                ts_op("dve", k_t[:], a_t[:], 1.0 / TWO_PI, None, ALU.mult, ALU.bypass, [a_k], [k_k])
                P.op("dve", lambda e, ki_t=ki_t, k_t=k_t: e.tensor_copy(out=ki_t[:], in_=k_t[:]), [k_k], ["ki"])
                P.op("dve", lambda e, ki_t=ki_t, k_t=k_t: e.tensor_copy(out=k_t[:], in_=ki_t[:]), ["ki"], [k_k])
                stt_op(a_t[:], k_t[:], -TWO_PI, a_t[:], ALU.mult, ALU.add, [k_k, a_k], [a_k])
                ts_op("dve", k_t[:], a_t[:], math.pi, -TWO_PI, ALU.is_gt, ALU.mult, [a_k], [k_k])
                tt_op("dve", a_t[:], a_t[:], k_t[:], ALU.add, [a_k, k_k], [a_k])
                ts_op("dve", k_t[:], a_t[:], -math.pi, TWO_PI, ALU.is_lt, ALU.mult, [a_k], [k_k])
                tt_op("dve", a_t[:], a_t[:], k_t[:], ALU.add, [a_k, k_k], [a_k])
                ts_op("dve", a_t[:], a_t[:], math.pi, -math.pi, ALU.min, ALU.max, [a_k], [a_k])
                act(k_t[:], a_t[:], AF.Sin, [a_k], [k_k])
                dst = ropeT[:, typ * 2 + which, tsl(tt)]
                dk_ = ("rope", typ, which, tt)
                if which == 0:
                    ts_op("dve", k_t[:], k_t[:], -1.0, msk, ALU.add, ALU.mult, [k_k, "consts"], [k_k])
                    ts_op("dve", dst, k_t[:], 1.0, None, ALU.add, ALU.bypass, [k_k], [dk_])
                else:
                    ts_op("dve", dst, k_t[:], sgn, None, ALU.mult, ALU.bypass, [k_k, "consts"], [dk_])

    def rope_keys(typ, tt):
        return [("rope", typ, 0, tt), ("rope", typ, 1, tt)]

    for i in range(2):
        P.op("pool", lambda e, i=i: e.memset(qst[i][:], 0.0), writes=[("qst", i)])

    def load_w(wdram2d, row0, nk, col0, ncols, queue="pool"):
        t, key = wring.next()
        dst = t[:, 0:nk * ncols].rearrange("p (k c) -> p k c", k=nk)
        src = wdram2d[row0:row0 + nk * 128, col0:col0 + ncols].rearrange("(k p) c -> p k c", p=128)
        P.dma(queue, dst, src, writes=[key])
        return dst, key

    epsd = {}
    epst = P.sbuf("epst", [128, 4], F32)
    for i, ev in enumerate((1e-6, 1e-5)):
        P.op("pool", lambda e, i=i, ev=ev: e.memset(epst[:, i:i + 1], ev), writes=["eps"])
        epsd[ev] = epst[:, i:i + 1]

    def rms_stats(src_ap, src_keys, nch, dim, eps, tt):
        b = psr2.next()
        for c in range(nch):
            st, sk = bsr.next()
            act(st[:], src_ap(c, tt), AF.Square, [src_keys(c, tt)], [sk])
            mm(ps[b][:], ones, st[:], c == 0, c == nch - 1, [sk, "ones"], [pk(b)])
        tmp, tk = scr.next()
        rt, rkey = scr.next()
        act(tmp[:], ps[b][:], AF.Sqrt, [pk(b), "eps"], [tk], bias=epsd[eps], scale=1.0 / dim)
        P.op("dve", lambda e: e.reciprocal(out=rt[:], in_=tmp[:]), [tk], [rkey])
        return rt, rkey

    def layer_norm_to_r1(l, gcol0):
        for tt in range(NTT):
            rt, rkey = rms_stats(lambda c, tt: xT[:, c, tsl(tt)], lambda c, tt: ("xT", c, tt), 16, float(D), 1e-6, tt)
            for c in range(16):
                stt_op(r1[:, c, tsl(tt)], xT[:, c, tsl(tt)], vecs[:, l * NV + gcol0 + c: l * NV + gcol0 + c + 1], rt[:],
                       ALU.mult, ALU.mult, [("xT", c, tt), "vecs", rkey], [("r1", c, tt)])

    def proj_fm(wt, wkey, nk, col_lo, src_ap, src_key, tt, bank, m=128):
        for k in range(nk):
            mm(ps[bank][0:m, :], wt[:, k, col_lo:col_lo + m], src_ap(k, tt), k == 0, k == nk - 1,
               [wkey, src_key(k, tt)], [pk(bank)])

    hT_ap = lambda k, tt: r1[:, k, tsl(tt)]
    hT_key = lambda k, tt: ("r1", k, tt)

    def evac_copy(bank, dst, dkeys, eng="act"):
        if eng == "act":
            act(dst, ps[bank][:], AF.Copy, [pk(bank)], dkeys)
        else:
            P.op("dve", lambda e: e.tensor_copy(out=dst, in_=ps[bank][:]), [pk(bank)], dkeys)

    def rope_evac(bx, bsw, typ, tt, dst, dkeys, m=128):
        t1, k1 = scr.next()
        t2, k2 = scr.next()
        tt_op("dve", t1[0:m, :], ps[bx][0:m, :], ropeT[0:m, typ * 2 + 0, tsl(tt)], ALU.mult, [pk(bx), ("rope", typ, 0, tt)], [k1])
        tt_op("dve", t2[0:m, :], ps[bsw][0:m, :], ropeT[0:m, typ * 2 + 1, tsl(tt)], ALU.mult, [pk(bsw), ("rope", typ, 1, tt)], [k2])
        tt_op("dve", dst, t1[0:m, :], t2[0:m, :], ALU.add, [k1, k2], dkeys)

    def make_sw(wt, wkey, nk, ncols, blk, half):
        st, sk = swring.next()
        sv = st[:, 0:nk * ncols].rearrange("p (k c) -> p k c", k=nk)
        P.op("dve", lambda e: e.memset(st[:, 0:nk * ncols], 0.0), [], [sk])
        nb = ncols // blk
        s4 = st[:, 0:nk * ncols].rearrange("p (k b d) -> p k b d", k=nk, b=nb)
        w4 = wt.rearrange("p k (b d) -> p k b d", b=nb)
        P.op("dve", lambda e: e.tensor_copy(out=s4[:, :, :, 0:half], in_=w4[:, :, :, half:2 * half]), [wkey, sk], [sk])
        P.op("dve", lambda e: e.tensor_copy(out=s4[:, :, :, half:2 * half], in_=w4[:, :, :, 0:half]), [wkey, sk], [sk])
        return sv, sk

    for l in range(depth):
        lam_init = 0.8 - 0.6 * math.exp(-0.3 * l)
        par = l % 2
        vb = l * NV
        W = w_in[l]
        kvk_d = []
        kvk_m = []

        def kv_write(which, r0, nr, c0, ncol, src_ap, skey):
            key = ("kv_loc", which, r0, c0)
            dst = (kvd_loc if which == 0 else kvm_loc)
            P.dma("sp", dst[r0:r0 + nr, c0:c0 + ncol], src_ap, reads=[skey], writes=[key])
            (kvk_d if which == 0 else kvk_m).append(key)

        def q_write(r0, tt, src_ap, skey):
            key = ("q_scr", r0, tt)
            P.dma("sp", q_scr[r0:r0 + 128, tsl(tt)], src_ap, reads=[skey], writes=[key])

        P.dma("sp", lamt[:], lams_in[l], writes=["lamt"])
        t1, k1 = scr.next()
        tt_op("dve", t1[:, 0:128], lamt[:, 0:128], lamt[:, 128:256], ALU.mult, ["lamt"], [k1])
        P.op("dve", lambda e, t1=t1: e.reduce_sum(out=small[:, 0:1], in_=t1[:, 0:64], axis=mybir.AxisListType.X), [k1], [("small", 0)])
        P.op("dve", lambda e, t1=t1: e.reduce_sum(out=small[:, 1:2], in_=t1[:, 64:128], axis=mybir.AxisListType.X), [k1], [("small", 1)])
        act(small[:, 0:2], small[:, 0:2], AF.Exp, [("small", 0), ("small", 1)], [("small", 0), ("small", 1)])
        tt_op("dve", small[:, 2:3], small[:, 1:2], small[:, 0:1], ALU.subtract, [("small", 0), ("small", 1)], [("small", 2)])
        ts_op("dve", small[:, 2:3], small[:, 2:3], -lam_init, None, ALU.add, ALU.bypass, [("small", 2)], [("small", 2)])
        ts_op("dve", small[:, 3:4], vecs[:, vb + 86: vb + 87], 1.0 - lam_init, None, ALU.mult, ALU.bypass, ["vecs"], [("small", 3)])
        act(small[:, 8:14], vecs[:, vb + 74: vb + 80], AF.Exp, ["vecs"], [("small", 8)], scale=-1.0)
        act(small[:, 8:14], small[:, 8:14], AF.Ln, [("small", 8)], [("small", 8)], bias=1.0)
        ts_op("dve", small[:, 8:14], small[:, 8:14], -8.0, None, ALU.mult, ALU.bypass, [("small", 8)], [("small", 8)])
        P.dma("pool", wg[:, 0:6, :], w_r[l].rearrange("h c d -> c h d"), writes=["wg"])
        P.dma("pool", wg[:, 6:12, :], w_i[l].rearrange("h c d -> c h d"), writes=["wg"])

        layer_norm_to_r1(l, 0)

        def do_chunk(col0, handler):
            wt, wkey = load_w(W, 0, 16, col0, 128)
            for tt in range(NTT):
                b = psr.next()
                proj_fm(wt, wkey, 16, 0, hT_ap, hT_key, tt, b)
                handler(col0 // 128, tt, b)

        def rope_chunk(col0, handler_dst):
            wt, wkey = load_w(W, 0, 16, col0, 128)
            sv, sk = make_sw(wt, wkey, 16, 128, 64, 8)
            for tt in range(NTT):
                bx = psr.next()
                bs_ = psr.next()
                proj_fm(wt, wkey, 16, 0, hT_ap, hT_key, tt, bx)
                proj_fm(sv, sk, 16, 0, hT_ap, hT_key, tt, bs_)
                st, stk = bsr.next()
                rope_evac(bx, bs_, 0, tt, st[:], [stk])
                handler_dst(col0 // 128, tt, st, stk)

        def h_lx(ch, tt, b):
            evac_copy(b, lx_ap(ch, 3 + tt * TT, 3 + (tt + 1) * TT), [("lx", ch, tt)], eng="act")
        for c0 in range(0, 768, 128):
            do_chunk(c0, h_lx)
        for ch in range(6):
            P.op("dve", lambda e, ch=ch: e.tensor_copy(out=htail[:, ch * 4: ch * 4 + 3], in_=lx_ap(ch, 1024, 1027)), [("lx", ch, 1), "htail"], ["htail"])
        hb, hk = bsr.next()
        P.op("dve", lambda e, hb=hb: e.tensor_copy(out=hb[:, 0:32], in_=htail[:, 0:32]), ["htail"], [hk])
        P.dma("sp", halo_loc[:, :], hb[:, 0:32], reads=[hk], writes=["halo_loc"])
        P.custom_dma("pool", lambda e, par=par: e.collective_compute(
            "AllGather", ALU.bypass, replica_groups=[list(range(NCORES))],
            ins=[halo_loc.ap().opt()], outs=[halo_all[par].ap().opt()]),
            reads=["halo_loc"], writes=[("halo_all", par)], inc=1)

        def h_dk(cidx, tt, st, stk):
            h = cidx - 16
            kv_write(0, h * 128, 128, tt * TT, TT, st[:], stk)
        for c0 in range(2048, 2560, 128):
            rope_chunk(c0, h_dk)
        for hd in range(4):
            wt, wkey = load_w(W, 0, 16, 2560 + hd * 128, 128)
            for t8 in range(8):
                b = psr.next()
                for k in range(16):
                    mm(ps[b][:, 0:128], r1[:, k, t8 * 128:(t8 + 1) * 128], wt[:, k, :], k == 0, k == 15,
                       [wkey, ("r1", k, t8 // 4)], [pk(b)])
                st, stk = bsr.next()
                act(st[:, 0:128], ps[b][:, 0:128], AF.Copy, [pk(b)], [stk])
                kv_write(0, 512 + hd * 128, 128, t8 * 128, 128, st[:, 0:128], stk)
        P.custom_dma("pool", lambda e, par=par: e.collective_compute(
            "AllGather", ALU.bypass, replica_groups=[list(range(NCORES))],
            ins=[kvd_loc.ap().opt()], outs=[kvd_all[par].ap().opt()]),
            reads=list(kvk_d), writes=[("kvd_all", par)], inc=1)

        def h_gy(cidx, tt, b):
            ch = cidx - 6
            act(gy_ap(ch, tt * TT, (tt + 1) * TT), ps[b][:], AF.Gelu_apprx_tanh, [pk(b)], [("gy", ch, tt)])
        for c0 in range(768, 1536, 128):
            do_chunk(c0, h_gy)

        def h_dq(cidx, tt, st, stk):
            h = cidx - 12
            q_write(h * 128, tt, st[:], stk)
        for c0 in range(1536, 2048, 128):
            rope_chunk(c0, h_dq)

        for cq in range(4):
            wt, wkey = load_w(W, 0, 16, 3072 + cq * 128, 128)
            for tt in range(NTT):
                b = psr.next()
                proj_fm(wt, wkey, 16, 0, hT_ap, hT_key, tt, b)
                evac_copy(b, qn[:, cq, tsl(tt)], [("qn", cq, tt)], eng="act")
        for tt in range(NTT):
            rt, rkey = rms_stats(lambda c, tt: qn[:, c, tsl(tt)], lambda c, tt: ("qn", c, tt), 4, 512.0, 1e-6, tt)
            for c in range(4):
                stt_op(qn[:, c, tsl(tt)], qn[:, c, tsl(tt)], vecs[:, vb + 80 + c: vb + 81 + c], rt[:],
                       ALU.mult, ALU.mult, [("qn", c, tt), "vecs", rkey], [("qn", c, tt)])
        qn_ap = lambda k, tt: qn[:, k, tsl(tt)]
        qn_key = lambda k, tt: ("qn", k, tt)
        WQ = w_q_b[l]
        for h in range(6):
            wq, key_ = load_w(WQ, 0, 4, h * 192, 128)
            for tt in range(NTT):
                b = psr.next()
                proj_fm(wq, key_, 4, 0, qn_ap, qn_key, tt, b)
                st, stk = bsr.next()
                evac_copy(b, st[:], [stk], eng="dve")
                q_write(512 + h * 128, tt, st[:], stk)
        for r in range(3):
            t_, key_ = wring.next()
            wq = t_[:, 0:4 * 128].rearrange("p (k c) -> p k c", k=4)
            for hh in range(2):
                h = 2 * r + hh
                P.dma("pool", wq[:, :, hh * 64:(hh + 1) * 64],
                      WQ[:, h * 192 + 128: h * 192 + 192].rearrange("(k p) c -> p k c", p=128), writes=[key_])
            sv, sk = make_sw(wq, key_, 4, 128, 64, 32)
            for tt in range(NTT):
                bx = psr.next()
                bs_ = psr.next()
                proj_fm(wq, key_, 4, 0, qn_ap, qn_key, tt, bx)
                proj_fm(sv, sk, 4, 0, qn_ap, qn_key, tt, bs_)
                st, stk = bsr.next()
                rope_evac(bx, bs_, 1, tt, st[:], [stk])
                q_write(1280 + r * 128, tt, st[:], stk)

        for j in range(2):
            wt, wkey = load_w(W, 0, 16, 3584 + j * 128, 128)
            for tt in range(NTT):
                b = psr.next()
                proj_fm(wt, wkey, 16, 0, hT_ap, hT_key, tt, b)
                evac_copy(b, kvn[:, j, tsl(tt)], [("qn", j, tt)], eng="act")
        for tt in range(NTT):
            rt, rkey = rms_stats(lambda c, tt: kvn[:, c, tsl(tt)], lambda c, tt: ("qn", c, tt), 2, 256.0, 1e-6, tt)
            for c in range(2):
                stt_op(kvn[:, c, tsl(tt)], kvn[:, c, tsl(tt)], vecs[:, vb + 84 + c: vb + 85 + c], rt[:],
                       ALU.mult, ALU.mult, [("qn", c, tt), "vecs", rkey], [("qn", c, tt)])
        wkr, key_ = load_w(W, 0, 16, 3840, 64)
        sv, sk = make_sw(wkr, key_, 16, 64, 64, 32)
        for tt in range(NTT):
            bx = psr.next()
            bs_ = psr.next()
            proj_fm(wkr, key_, 16, 0, hT_ap, hT_key, tt, bx, m=64)
            proj_fm(sv, sk, 16, 0, hT_ap, hT_key, tt, bs_, m=64)
            st, stk = bsr.next()
            rope_evac(bx, bs_, 1, tt, st[0:64, :], [stk], m=64)
            kv_write(1, 768, 64, tt * TT, TT, st[0:64, :], stk)
        WKV = w_kv_b[l]
        kvn_ap = lambda k, tt: kvn[:, k, tsl(tt)]
        kvn_key = lambda k, tt: ("qn", k, tt)
        for h in range(6):
            wkv, key_ = load_w(WKV, 0, 2, h * 256, 256)
            for tt in range(NTT):
                b = psr.next()
                proj_fm(wkv, key_, 2, 0, kvn_ap, kvn_key, tt, b)
                st, stk = bsr.next()
                evac_copy(b, st[:], [stk], eng="dve")
                kv_write(1, h * 128, 128, tt * TT, TT, st[:], stk)
            for t8 in range(8):
                b = psr.next()
                for k in range(2):
                    mm(ps[b][:, 0:128], kvn[:, k, t8 * 128:(t8 + 1) * 128], wkv[:, k, 128:256], k == 0, k == 1,
                       [key_, ("qn", k, t8 // 4)], [pk(b)])
                st, stk = bsr.next()
                act(st[:, 0:128], ps[b][:, 0:128], AF.Copy, [pk(b)], [stk])
                kv_write(1, 832 + h * 128, 128, t8 * 128, 128, st[:, 0:128], stk)
        P.custom_dma("pool", lambda e, par=par: e.collective_compute(
            "AllGather", ALU.bypass, replica_groups=[list(range(NCORES))],
            ins=[kvm_loc.ap().opt()], outs=[kvm_all[par].ap().opt()]),
            reads=list(kvk_m), writes=[("kvm_all", par)], inc=1)

        def lru_halo():
            P.dma("sp", halt[:], halo_all[par].ap().rearrange("(r p) c -> p r c", p=128), reads=[("halo_all", par)], writes=["halt"])
            ts_op("dve", hsel[:], halt[:, 0, :], sel[:, 0:1], None, ALU.mult, ALU.bypass, ["halt", "sel"], ["hsel"])
            for r in range(1, NCORES):
                stt_op(hsel[:], halt[:, r, :], sel[:, r:r + 1], hsel[:], ALU.mult, ALU.add, ["halt", "sel", "hsel"], ["hsel"])
            for ch in range(6):
                P.op("dve", lambda e, ch=ch: e.tensor_copy(out=lx_ap(ch, 0, 3), in_=hsel[:, ch * 4: ch * 4 + 3]), ["hsel"], [("lxh", ch)])

        lt = lt_t
        ltk = [f"sc{i}" for i in range(NLT)]

        def lru_iter(ch, tt):
            o = tt * TT
            rk = [("lx", ch, tt), ("lxh", ch), "vecs"] + ([("lx", ch, 0)] if tt == 1 else [])
            xc, xck = lt[0], ltk[0]
            cw = lambda j: vecs[:, vb + 32 + ch * 4 + j: vb + 33 + ch * 4 + j]
            ts_op("dve", xc[:], lx_ap(ch, o, o + TT), cw(0), vecs[:, vb + 56 + ch: vb + 57 + ch], ALU.mult, ALU.add, rk, [xck])
            for j in range(1, 4):
                stt_op(xc[:], lx_ap(ch, o + j, o + j + TT), cw(j), xc[:], ALU.mult, ALU.add, rk + [xck], [xck])
            xb, xbk = bsr.next()
            act(xb[:], xc[:], AF.Copy, [xck], [xbk])
            br = psr2.next()
            bi = psr2.next()
            mm(ps[br][:], wg[:, ch, :], xb[:], True, True, ["wg", xbk], [pk(br)])
            mm(ps[bi][:], wg[:, 6 + ch, :], xb[:], True, True, ["wg", xbk], [pk(bi)])
            ra, rak = lt[1], ltk[1]
            ii, iik = lt[2], ltk[2]
            act(ra[:], ps[br][:], AF.Sigmoid, [pk(br), "vecs"], [rak], bias=vecs[:, vb + 62 + ch: vb + 63 + ch])
            act(ii[:], ps[bi][:], AF.Sigmoid, [pk(bi), "vecs"], [iik], bias=vecs[:, vb + 68 + ch: vb + 69 + ch])
            act(ra[:], ra[:], AF.Exp, [rak, ("small", 8)], [rak], scale=small[:, 8 + ch: 9 + ch])
            tm, tmk = lt[3], ltk[3]
            tt_op("dve", tm[:], ra[:], ra[:], ALU.mult, [rak], [tmk])
            ts_op("dve", tm[:], tm[:], -1.0, 1.0, ALU.mult, ALU.add, [tmk], [tmk])
            act(tm[:], tm[:], AF.Sqrt, [tmk], [tmk])
            tt_op("dve", ii[:], ii[:], xc[:], ALU.mult, [iik, xck], [iik])
            tt_op("dve", ii[:], ii[:], tm[:], ALU.mult, [iik, tmk], [iik])
            hl, hlk = lt[4], ltk[4]
            aa, aak = lt[5], ltk[5]
            init_h = 0.0 if tt == 0 else small[:, 16 + ch: 17 + ch]
            init_a = 1.0 if tt == 0 else small[:, 24 + ch: 25 + ch]
            rr = [rak, iik] + ([("small", 16, ch)] if tt == 1 else [])
            P.op("dve", lambda e: e.tensor_tensor_scan(
                out=hl[:], data0=ra[:], data1=ii[:], initial=init_h, op0=ALU.mult, op1=ALU.add), rr, [hlk])
            rr2 = [rak, "ones"] + ([("small", 24, ch)] if tt == 1 else [])
            P.op("dve", lambda e: e.tensor_tensor_scan(
                out=aa[:], data0=ra[:], data1=ones_t[:], initial=init_a, op0=ALU.mult, op1=ALU.mult), rr2, [aak])
            P.op("dve", lambda e: e.tensor_copy(out=small[:, 16 + ch: 17 + ch], in_=hl[:, TT - 1: TT]), [hlk], [("small", 16, ch)])
            P.op("dve", lambda e: e.tensor_copy(out=small[:, 24 + ch: 25 + ch], in_=aa[:, TT - 1: TT]), [aak], [("small", 24, ch)])
            p1w = [("p1", ch, tt), ("lx", ch, tt), ("lxh", ch)] + ([("lx", ch, 0)] if tt == 1 else [])
            tt_op("dve", p1_ap(ch, o, o + TT), hl[:], gy_ap(ch, o, o + TT), ALU.mult, [hlk, ("gy", ch, tt)], p1w)
            tt_op("dve", gy_ap(ch, o, o + TT), aa[:], gy_ap(ch, o, o + TT), ALU.mult, [aak, ("gy", ch, tt)], [("gy", ch, tt)])

        def lru_carry_trigger():
            P.op("dve", lambda e: e.memset(carl[:], 0.0), [], ["carl"])
            P.op("dve", lambda e: e.tensor_copy(out=carl[:, 0:6], in_=small[:, 16:22]), [("small", 16, ch) for ch in range(6)] + ["carl"], ["carl"])
            P.op("dve", lambda e: e.tensor_copy(out=carl[:, 6:12], in_=small[:, 24:30]), [("small", 24, ch) for ch in range(6)] + ["carl"], ["carl"])
            P.dma("sp", carry_loc[:, :], carl[:], reads=["carl"], writes=["carry_loc"])
            P.custom_dma("pool", lambda e, par=par: e.collective_compute(
                "AllGather", ALU.bypass, replica_groups=[list(range(NCORES))],
                ins=[carry_loc.ap().opt()], outs=[carry_all[par].ap().opt()]),
                reads=["carry_loc"], writes=[("carry_all", par)], inc=1)

        def lru_finish():
            P.dma("sp", cart[:], carry_all[par].ap().rearrange("(r p) c -> p r c", p=128), reads=[("carry_all", par)], writes=["cart"])
            Hh = small[:, 32:38]
            tA = small[:, 40:46]
            P.op("dve", lambda e: e.memset(Hh, 0.0), [], ["H"])
            for r in range(NCORES):
                tt_op("dve", tA, cart[:, r, 6:12], Hh, ALU.mult, ["cart", "H"], ["tA"])
                tt_op("dve", tA, tA, cart[:, r, 0:6], ALU.add, ["cart", "tA"], ["tA"])
                tt_op("dve", tA, tA, Hh, ALU.subtract, ["tA", "H"], ["tA"])
                stt_op(Hh, tA, sel[:, 8 + r: 9 + r], Hh, ALU.mult, ALU.add, ["tA", "sel", "H"], ["H"])
            for ch in range(6):
                for tt in range(NTT):
                    o = tt * TT
                    stt_op(r1[:, ch, tsl(tt)], gy_ap(ch, o, o + TT), small[:, 32 + ch: 33 + ch], p1_ap(ch, o, o + TT), ALU.mult, ALU.add,
                           [("gy", ch, tt), ("p1", ch, tt), "H"], [("r1", ch, tt)])

        KD = kvd_all[par]
        KM = kvm_all[par]
        sc_d = 64.0 ** -0.5
        sc_m = 192.0 ** -0.5

        def load_stage(hd, rb):
            s = strr.next()
            if hd < 4:
                base = rb * KDR
                kq = [("kvd_all", par)]
                src = KD[base + hd * 128: base + (hd + 1) * 128, :].rearrange("(m r) c -> r m c", m=2)
                P.dma("sp", kst[s][0:64, :, :], src, reads=kq, writes=[("kst", s, 0)])
                for m in range(2):
                    P.dma("sp", kst[s][64:80, m, :], U_in[:, rb * T:(rb + 1) * T], writes=[("kst", s, 1 + m)])
                P.dma("sp", vst[s][:].rearrange("p t d -> p (t d)"), KD[base + 512 + hd * 128: base + 512 + (hd + 1) * 128, :],
                      reads=kq, writes=[("vst", s)])
            else:
                h = hd - 4
                base = rb * KMR
                kq = [("kvm_all", par)]
                P.dma("sp", kst[s][:, 0, :], KM[base + h * 128: base + (h + 1) * 128, :], reads=kq, writes=[("kst", s, 0)])
                P.dma("sp", kst[s][0:64, 1, :], KM[base + 768: base + 832, :], reads=kq, writes=[("kst", s, 1)])
                P.dma("sp", kst[s][64:80, 1, :], U_in[:, rb * T:(rb + 1) * T], writes=[("kst", s, 2)])
                P.dma("sp", vst[s][:].rearrange("p t d -> p (t d)"), KM[base + 832 + h * 128: base + 832 + (h + 1) * 128, :],
                      reads=kq, writes=[("vst", s)])
            return s

        def kst_keys(s):
            return [("kst", s, 0), ("kst", s, 1), ("kst", s, 2)]

        def load_q(hd, qt):
            qi = qstr.next()
            if hd < 4:
                src = q_scr[hd * 128:(hd + 1) * 128, tsl(qt)].rearrange("(m r) c -> r m c", m=2)
                P.dma("sp", qst[qi][0:64, :, :], src, reads=[("q_scr", hd * 128, qt)], writes=[("qst", qi, 0)])
                for m in range(2):
                    P.dma("sp", qst[qi][64:80, m, :], Vm_in[:, tsl(qt)], writes=[("qst", qi, 1 + m)])
            else:
                h = hd - 4
                P.dma("sp", qst[qi][:, 0, :], q_scr[512 + h * 128: 512 + (h + 1) * 128, tsl(qt)], reads=[("q_scr", 512 + h * 128, qt)], writes=[("qst", qi, 0)])
                P.dma("sp", qst[qi][0:64, 1, :], q_scr[1280 + h * 64: 1280 + (h + 1) * 64, tsl(qt)], reads=[("q_scr", 1280 + (h // 2) * 128, qt)], writes=[("qst", qi, 1)])
                P.dma("sp", qst[qi][64:80, 1, :], Vm_in[:, tsl(qt)], writes=[("qst", qi, 2)])
            return qi

        def qst_keys(qi):
            return [("qst", qi, 0), ("qst", qi, 1), ("qst", qi, 2)]

        lru_halo()
        deferred = [None]
        unit = 0
        for hd in range(10):
            isd = hd < 4
            for qt in range(NTT):
                if unit < 12:
                    lru_iter(unit // 2, unit % 2)
                if unit == 12:
                    lru_carry_trigger()
                if unit == 15:
                    lru_finish()
                unit += 1
                qi = load_q(hd, qt)
                pend = None
                stage = load_stage(hd, 0)
                for rb in range(NCORES):
                    nstage = None
                    s = stage
                    for t in range(8):
                        first = (rb == 0 and t == 0)
                        last = (rb == NCORES - 1 and t == 7)
                        ksl = slice(t * 128, (t + 1) * 128)
                        cur = []
                        rdk = kst_keys(s) + qst_keys(qi)
                        if isd:
                            for m in range(2):
                                b = psr.next()
                                mm(ps[b][:], kst[s][0:80, m, ksl], qst[qi][0:80, m, :], True, True, rdk, [pk(b)])
                                cur.append((b, m))
                        else:
                            b = psr.next()
                            mm(ps[b][:], kst[s][:, 0, ksl], qst[qi][:, 0, :], True, False, rdk, [pk(b)])
                            mm(ps[b][:], kst[s][0:80, 1, ksl], qst[qi][0:80, 1, :], False, True, rdk, [pk(b)])
                            cur.append((b, 0))
                        ptl = []
                        for (b, m) in cur:
                            pt, ptk = ptr.next()
                            act(pt[:], ps[b][:], AF.Exp, [pk(b)], [ptk], scale=(sc_d if isd else sc_m))
                            ptl.append((pt, ptk, m))
                        if pend is not None:
                            pend()
                        if deferred[0] is not None and rb == 0 and t == 2:
                            deferred[0]()
                            deferred[0] = None
                        if t == 0 and rb + 1 < NCORES:
                            nstage = load_stage(hd, rb + 1)
                        def do_pv(ptl=ptl, s=s, t=t, first=first, last=last):
                            for (pt, ptk, m) in ptl:
                                mm(ps[0 + m][:], vst[s][:, t, :], pt[:], first, last, [("vst", s), ptk], [pk(0 + m)])
                                mm(ps[2 + m][:], ones, pt[:], first, last, ["ones", ptk], [pk(2 + m)])
                        pend = do_pv
                    stage = nstage
                pend()
                A, Ak = scr_att.next()
                if isd:
                    B, Bk = scr_att.next()
                    P.op("dve", lambda e, A=A: e.reciprocal(out=A[:], in_=ps[2][:]), [pk(2)], [Ak])
                    tt_op("dve", B[:], ps[0][:], A[:], ALU.mult, [pk(0), Ak], [Bk])
                    P.op("dve", lambda e, A=A: e.reciprocal(out=A[:], in_=ps[3][:]), [pk(3), Bk], [Ak])
                    tt_op("dve", A[:], ps[1][:], A[:], ALU.mult, [pk(1), Ak], [Ak])
                    stt_op(B[:], A[:], small[:, 2:3], B[:], ALU.mult, ALU.add, [Ak, Bk, ("small", 2)], [Bk])
                    def fin_tail(A=A, Ak=Ak, B=B, Bk=Bk, hd=hd, qt=qt):
                        sq, sqk = bsr.next()
                        act(sq[:], B[:], AF.Square, [Bk], [sqk])
                        b = psr.next()
                        mm(ps[b][:], ones, sq[:], True, True, [sqk, "ones"], [pk(b)])
                        act(A[:], ps[b][:], AF.Sqrt, [pk(b), "eps", Ak], [Ak], bias=epsd[1e-5], scale=1.0 / 128.0)
                        P.op("dve", lambda e: e.reciprocal(out=A[:], in_=A[:]), [Ak], [Ak])
                        stt_op(r1[:, 6 + hd, tsl(qt)], B[:], small[:, 3:4], A[:], ALU.mult, ALU.mult,
                               [Bk, ("small", 3), Ak], [("r1", 6 + hd, qt)])
                    assert deferred[0] is None
                    deferred[0] = fin_tail
                else:
                    P.op("dve", lambda e, A=A: e.reciprocal(out=A[:], in_=ps[2][:]), [pk(2)], [Ak])
                    tt_op("dve", r1[:, 6 + hd, tsl(qt)], ps[0][:], A[:], ALU.mult, [pk(0), Ak], [("r1", 6 + hd, qt)])

        if deferred[0] is not None:
            deferred[0]()
            deferred[0] = None
        if debug == 1 and l == depth - 1:
            dbg = nc.dram_tensor("dbg", [D, T], BF16, kind="ExternalOutput")
            for c in range(16):
                for tt in range(NTT):
                    P.dma("sp", dbg[c * 128:(c + 1) * 128, tsl(tt)], r1[:, c, tsl(tt)], reads=[("r1", c, tt)])
            P.build()
            return nc
        WO = w_out[l]
        mix_ap = lambda k, tt: r1[:, k, tsl(tt)]
        mix_key = lambda k, tt: ("r1", k, tt)
        for oc in range(16):
            wt, wkey = load_w(WO, 0, 16, oc * 128, 128)
            for tt in range(NTT):
                b = psr.next()
                proj_fm(wt, wkey, 16, 0, mix_ap, mix_key, tt, b)
                tt_op("dve", xT[:, oc, tsl(tt)], xT[:, oc, tsl(tt)], ps[b][:], ALU.add, [pk(b), ("xT", oc, tt)], [("xT", oc, tt)])

        layer_norm_to_r1(l, 16)
        WG, WU, WD = w_gate[l], w_up[l], w_down[l]
        NG = 4
        GC = 11
        for g in range(NG):
            for j in range(GC):
                col = (g * GC + j) * 128
                wtg, kg = load_w(WG, 0, 16, col, 128)
                wtu, ku = load_w(WU, 0, 16, col, 128)
                for tt in range(NTT):
                    bg = psr.next()
                    bu = psr.next()
                    proj_fm(wtg, kg, 16, 0, hT_ap, hT_key, tt, bg)
                    proj_fm(wtu, ku, 16, 0, hT_ap, hT_key, tt, bu)
                    s1, sk1 = scr.next()
                    act(s1[:], ps[bg][:], AF.Silu, [pk(bg)], [sk1])
                    tt_op("dve", g_ap(j, tt * TT, (tt + 1) * TT), s1[:], ps[bu][:], ALU.mult, [sk1, pk(bu)], [("g", j, tt)])
            for oc in range(16):
                wt, wkey = load_w(WD, g * GC * 128, GC, oc * 128, 128)
                for tt in range(NTT):
                    b = psr.next()
                    for k in range(GC):
                        mm(ps[b][:], wt[:, k, :], g_ap(k, tt * TT, (tt + 1) * TT), k == 0, k == GC - 1,
                           [wkey, ("g", k, tt)], [pk(b)])
                    tt_op("dve", xT[:, oc, tsl(tt)], xT[:, oc, tsl(tt)], ps[b][:], ALU.add, [pk(b), ("xT", oc, tt)], [("xT", oc, tt)])

    for tt in range(NTT):
        rt, rkey = rms_stats(lambda c, tt: xT[:, c, tsl(tt)], lambda c, tt: ("xT", c, tt), 16, float(D), 1e-6, tt)
        for c in range(16):
            stt_op(xT[:, c, tsl(tt)], xT[:, c, tsl(tt)], gfin[:, c:c + 1], rt[:], ALU.mult, ALU.mult,
                   [("xT", c, tt), "gfin", rkey], [("xT", c, tt)])
            P.dma("sp", outT[c * 128:(c + 1) * 128, tsl(tt)], xT[:, c, tsl(tt)], reads=[("xT", c, tt)])
    P.build()
    return nc


def _fm(v):
    return np.ascontiguousarray(v.reshape(-1, 128).T)


def make_inputs(inputs, depth=DEPTH):
    x = np.asarray(inputs["x"], np.float32)[0]
    pos = np.asarray(inputs["positions"], np.int32)[0]
    f32 = lambda k: np.asarray(inputs[k], np.float32)
    consts = np.zeros((128, 8), np.float32)
    p = np.arange(128)
    i_d = (p % 64) % 8
    consts[:, 0] = (np.float32(500000.0) ** (-(i_d.astype(np.float32)) / np.float32(8))).astype(np.float32)
    i_m = (p % 64) % 32
    consts[:, 1] = (np.float32(500000.0) ** (-(i_m.astype(np.float32)) / np.float32(32))).astype(np.float32)
    pd = p % 64
    consts[:, 2] = np.where(pd < 8, -1.0, np.where(pd < 16, 1.0, 0.0))
    consts[:, 3] = np.where(pd < 16, 1.0, 0.0)
    consts[:, 4] = np.where(pd < 32, -1.0, 1.0)
    consts[:, 5] = 1.0
    vecs = np.zeros((depth, 128, NV), np.float32)
    lams = np.zeros((depth, 128, 256), np.float32)
    for l in range(depth):
        vecs[l, :, 0:16] = _fm(f32("g_mix")[l])
        vecs[l, :, 16:32] = _fm(f32("g_ffn")[l])
        cw = f32("conv_w")[l]
        for ch in range(6):
            for j in range(4):
                vecs[l, :, 32 + ch * 4 + j] = cw[j, ch * 128:(ch + 1) * 128]
        vecs[l, :, 56:62] = _fm(f32("conv_b")[l])
        vecs[l, :, 62:68] = _fm(f32("b_r")[l])
        vecs[l, :, 68:74] = _fm(f32("b_i")[l])
        vecs[l, :, 74:80] = _fm(f32("lru_lambda")[l])
        vecs[l, :, 80:84] = _fm(f32("g_q_a")[l])
        vecs[l, :, 84:86] = _fm(f32("g_kv_a")[l])
        vecs[l, :, 86] = f32("g_sub")[l]
        lams[l, :, 0:64] = f32("lam_q1")[l][None, :]
        lams[l, :, 64:128] = f32("lam_q2")[l][None, :]
        lams[l, :, 128:192] = f32("lam_k1")[l][None, :]
        lams[l, :, 192:256] = f32("lam_k2")[l][None, :]
    gfin = _fm(f32("g_final"))
    shared = {
        "consts_in": consts, "vecs_in": vecs, "lams_in": lams, "gfin_in": gfin,
    }
    for k in ("w_in", "w_r", "w_i", "w_q_b", "w_kv_b", "w_out", "w_gate", "w_up", "w_down"):
        shared[k] = np.ascontiguousarray(f32(k)[:depth])
    kk = np.arange(S)
    rk = kk // T
    wck = (kk % T) // 64
    qq = np.arange(T)
    wcq = qq // 64
    maps = []
    for c in range(NCORES):
        U = np.zeros((16, S), np.float32)
        Vm = np.zeros((16, T), np.float32)
        U[0] = (rk > c)
        Vm[0] = -BIG
        for j in range(1, 16):
            U[j] = (rk == c) & (wck == j)
            Vm[j] = np.where(wcq < j, -BIG, 0.0)
        sel = np.zeros((128, 16), np.float32)
        if c > 0:
            sel[:, c - 1] = 1.0
        for r in range(NCORES):
            sel[:, 8 + r] = 1.0 if r < c else 0.0
        m = dict(shared)
        m["xT_in"] = np.ascontiguousarray(x[c * T:(c + 1) * T, :].T)
        m["pos_in"] = np.ascontiguousarray(np.broadcast_to(pos[c * T:(c + 1) * T][None, :], (128, T))).astype(np.int32)
        m["U_in"] = U.astype(ml_dtypes.bfloat16)
        m["Vm_in"] = Vm.astype(ml_dtypes.bfloat16)
        m["sel_in"] = sel
        maps.append(m)
    return maps


_NC_CACHE = {}


def kernel(**inputs):
    depth = DEPTH
    if depth not in _NC_CACHE:
        _NC_CACHE[depth] = build_program(depth)
    nc = _NC_CACHE[depth]
    maps = make_inputs(inputs, depth)
    res = run_bass_kernel_spmd(nc, maps, core_ids=list(range(NCORES)))
    out = np.zeros((1, S, D), np.float32)
    for c in range(NCORES):
        out[0, c * T:(c + 1) * T, :] = np.asarray(res.results[c]["outT"], np.float32).T
    return out
```
